# Optimizing a Trainium2 kernel written in Bass

```python
import jax, jax.numpy as jnp
from jax import lax
import numpy as np

D_MODEL = 2048
BATCH = 16
SEQ = 2048
DEPTH = 4

D_MIX = D_MODEL
D_CONV = D_MIX // 2
D_HGRN = D_MIX - D_CONV
CONV_GROUPS = 8
CONV_GROUP_DIM = D_CONV // CONV_GROUPS
CONV_WIDTH = 3
HGRN_HEADS = 8
HGRN_KEY_DIM = D_HGRN // HGRN_HEADS
HGRN_VAL_DIM = D_HGRN // HGRN_HEADS
CHUNK = 64
D_IN = 3 * D_CONV + 4 * D_HGRN
D_FF = -(-8 * D_MODEL // (3 * 256)) * 256
EPS = 1e-6
MIN_FORGET = 1e-30

kernel_name = "hybrid_conv_hgrn2_parallel_trunk"


def rms_norm(x, gain):
    xf = x.astype(jnp.float32)
    y = xf * lax.rsqrt(jnp.mean(xf * xf, axis=-1, keepdims=True) + EPS)
    return (y * gain.astype(jnp.float32)).astype(x.dtype)


def short_conv_mixer(gate_b, gate_c, h, conv_w, gn):
    bsz, length, _ = h.shape
    u = gate_c * h
    up = jnp.pad(u, ((0, 0), (CONV_WIDTH - 1, 0), (0, 0)))
    y = up[:, 0:length] * conv_w[0]
    for j in range(1, CONV_WIDTH):
        y = y + up[:, j:j + length] * conv_w[j]
    y = gate_b * y
    y = rms_norm(y.reshape(bsz, length, CONV_GROUPS, CONV_GROUP_DIM),
                 gn.reshape(CONV_GROUPS, CONV_GROUP_DIM))
    return y.reshape(bsz, length, D_CONV)


def hgrn2_mixer(q, z, v, g, lb, gn):
    bsz, length, _ = q.shape
    n_chunks = length // CHUNK
    out_dtype = v.dtype
    qf = jax.nn.silu(q.astype(jnp.float32))
    zf = z.astype(jnp.float32)
    lbf = lb.astype(jnp.float32)
    sig = jax.nn.sigmoid(zf)
    f = lbf + (1.0 - lbf) * sig
    log_f = jnp.log(jnp.maximum(f, MIN_FORGET))
    k = (1.0 - lbf) * jax.nn.sigmoid(-zf)
    vf = v.astype(jnp.float32)

    def to_chunks(t, d):
        return t.reshape(bsz, n_chunks, CHUNK, HGRN_HEADS, d).transpose(1, 0, 3, 2, 4)

    qc = to_chunks(qf, HGRN_KEY_DIM)
    kc = to_chunks(k, HGRN_KEY_DIM)
    lfc = to_chunks(log_f, HGRN_KEY_DIM)
    vc = to_chunks(vf, HGRN_VAL_DIM)
    causal = jnp.tril(jnp.ones((CHUNK, CHUNK), dtype=bool))[:, :, None]

    def step(state, inp):
        qi, ki, vi, lfi = inp
        b = jnp.cumsum(lfi, axis=-2)
        inter = jnp.einsum('bhtk,bhkv->bhtv', qi * jnp.exp(b), state)
        diff = b[:, :, :, None, :] - b[:, :, None, :, :]
        decay = jnp.where(causal, jnp.exp(jnp.minimum(diff, 0.0)), 0.0)
        scores = jnp.einsum('bhtk,bhsk,bhtsk->bhts', qi, ki, decay)
        o = inter + jnp.einsum('bhts,bhsv->bhtv', scores, vi)
        b_last = b[:, :, -1:, :]
        new_state = (jnp.exp(b_last[:, :, 0, :])[..., None] * state
                     + jnp.einsum('bhsk,bhsv->bhkv', ki * jnp.exp(b_last - b), vi))
        return new_state, o

    s0 = jnp.zeros((bsz, HGRN_HEADS, HGRN_KEY_DIM, HGRN_VAL_DIM), jnp.float32)
    _, oc = lax.scan(step, s0, (qc, kc, vc, lfc))
    o = oc.transpose(1, 0, 3, 2, 4).reshape(bsz, length, HGRN_HEADS, HGRN_VAL_DIM)
    o = rms_norm(o, gn.reshape(HGRN_HEADS, HGRN_VAL_DIM)).reshape(bsz, length, D_HGRN)
    return (o * jax.nn.silu(g.astype(jnp.float32))).astype(out_dtype)


def setup_inputs(seed: int = 0) -> dict:
    key = jax.random.key(seed)
    ks = jax.random.split(key, 16)
    f32 = jnp.float32

    def nrm(k, shape, scale):
        return jax.random.normal(k, shape, f32) * scale

    return {
        "x": jax.random.normal(ks[0], (BATCH, SEQ, D_MODEL), f32),
        "norm_mix": 1.0 + nrm(ks[1], (DEPTH, D_MODEL), 0.02),
        "w_in": nrm(ks[2], (DEPTH, D_MODEL, D_IN), D_MODEL ** -0.5),
        "conv_w": nrm(ks[3], (DEPTH, CONV_WIDTH, D_CONV), CONV_WIDTH ** -0.5),
        "gn_conv": 1.0 + nrm(ks[4], (DEPTH, D_CONV), 0.02),
        "lb_logits": nrm(ks[5], (DEPTH, D_HGRN), 0.1),
        "gn_hgrn": 1.0 + nrm(ks[6], (DEPTH, D_HGRN), 0.02),
        "w_out": nrm(ks[7], (DEPTH, D_MIX, D_MODEL), D_MIX ** -0.5),
        "norm_ffn": 1.0 + nrm(ks[8], (DEPTH, D_MODEL), 0.02),
        "w_gate": nrm(ks[9], (DEPTH, D_MODEL, D_FF), D_MODEL ** -0.5),
        "w_up": nrm(ks[10], (DEPTH, D_MODEL, D_FF), D_MODEL ** -0.5),
        "w_down": nrm(ks[11], (DEPTH, D_FF, D_MODEL), D_FF ** -0.5),
        "norm_final": 1.0 + nrm(ks[12], (D_MODEL,), 0.02),
    }


def reference(x, norm_mix, w_in, conv_w, gn_conv, lb_logits, gn_hgrn, w_out,
              norm_ffn, w_gate, w_up, w_down, norm_final):
    p = jax.nn.softmax(lb_logits.astype(jnp.float32), axis=0)
    lower_bounds = jnp.clip(jnp.cumsum(p, axis=0) - p[0], 0.0, 1.0 - 1e-4)

    c0, c1, c2 = D_CONV, 2 * D_CONV, 3 * D_CONV
    h0, h1, h2 = c2 + D_HGRN, c2 + 2 * D_HGRN, c2 + 3 * D_HGRN
    for l in range(DEPTH):
        h = rms_norm(x, norm_mix[l])
        proj = h @ w_in[l]
        conv_out = short_conv_mixer(proj[..., :c0], proj[..., c0:c1], proj[..., c1:c2],
                                    conv_w[l], gn_conv[l])
        hgrn_out = hgrn2_mixer(proj[..., c2:h0], proj[..., h0:h1], proj[..., h1:h2],
                               proj[..., h2:], lower_bounds[l], gn_hgrn[l])
        mixed = jnp.concatenate([conv_out, hgrn_out], axis=-1)
        x = x + mixed @ w_out[l]
        h = rms_norm(x, norm_ffn[l])
        x = x + (jax.nn.silu(h @ w_gate[l]) * (h @ w_up[l])) @ w_down[l]
    return rms_norm(x, norm_final)
```

```python
import contextlib
import numpy as np
import concourse.bass as bass
import concourse.mybir as mybir
from concourse.bass_utils import run_bass_kernel_spmd

F32 = mybir.dt.float32
BF16 = mybir.dt.bfloat16
AF = mybir.ActivationFunctionType
ALU = mybir.AluOpType

D = 2048
KC = 16
NJ = 44
NQ = 4
JQ = 11
TT = 512
TP = 1024
LFULL = 4
EPS = 1e-6
NST = 2
NBF = 5
NF = 16
NH = 7
FW = 520
NPRM = 336

ENGS = ("pe", "act", "dve", "pool", "sp")
DMA_BW = 170e3
DMA_LAT = 2.0
PE_SLACK = 2.0


class Buf:
    __slots__ = ("name", "lw", "rd")

    def __init__(self, name):
        self.name = name
        self.lw = None
        self.rd = []


class Op:
    __slots__ = ("idx", "eng", "emit", "dur", "preds", "succs", "npred", "ready", "start",
                 "finish", "dma", "dma_sem", "dma_val", "dma_bytes", "need_sig", "sigval")


class Sched:
    def __init__(self):
        self.ops = []
        self.dma_counts = {}

    def add(self, eng, emit, reads=(), writes=(), dur=0.3, dma_sem=None, dma_bytes=0):
        op = Op()
        op.idx = len(self.ops)
        op.eng = eng
        op.emit = emit
        op.dur = dur
        op.preds = {}
        op.succs = []
        op.ready = 0.0
        op.need_sig = False
        op.sigval = 0
        op.dma = dma_sem is not None
        op.dma_sem = dma_sem
        op.dma_bytes = dma_bytes
        if op.dma:
            k = self.dma_counts.get(id(dma_sem), 0) + 1
            self.dma_counts[id(dma_sem)] = k
            op.dma_val = 16 * k
        else:
            op.dma_val = 0
        for b in reads:
            if b.lw is not None:
                self._dep(op, b.lw, "raw")
        for b in writes:
            if b.lw is not None:
                self._dep(op, b.lw, "waw")
            for r in b.rd:
                if r is not op:
                    self._dep(op, r, "war")
        for b in reads:
            b.rd.append(op)
        for b in writes:
            b.lw = op
            b.rd = []
        op.npred = len(op.preds)
        self.ops.append(op)
        return op

    def _dep(self, op, p, kind):
        if p is op:
            return
        if p.dma:
            sync = True
        elif p.eng != op.eng:
            sync = True
        else:
            sync = p.eng != "pe"
        if sync:
            p.need_sig = True
        if p in op.preds:
            op.preds[p] = op.preds[p] or sync
        else:
            op.preds[p] = sync
            p.succs.append(op)

    def schedule(self):
        ready = {e: [] for e in ENGS}
        eng_free = {e: 0.0 for e in ENGS}
        dma_free = 0.0
        order = {e: [] for e in ENGS}
        for op in self.ops:
            if op.npred == 0:
                ready[op.eng].append(op)
        remaining = len(self.ops)
        while remaining:
            best = None
            for e in ENGS:
                lst = ready[e]
                if not lst:
                    continue
                ef = eng_free[e]
                c = None
                ck = None
                for o in lst:
                    r = o.ready + PE_SLACK if (e == "pe" and o.dur < 2.0) else o.ready
                    k = (max(r, ef), o.idx)
                    if ck is None or k < ck:
                        ck = k
                        c = o
                if best is None or ck < best[0]:
                    best = (ck, c)
            assert best is not None, "deadlock in schedule"
            (s, _), op = best
            ready[op.eng].remove(op)
            s = max(op.ready, eng_free[op.eng])
            op.start = s
            if op.dma:
                issue_end = s + 0.06
                dstart = max(issue_end, dma_free)
                dfin = dstart + op.dma_bytes / DMA_BW
                dma_free = dfin
                op.finish = dfin + DMA_LAT
                eng_free[op.eng] = issue_end
            else:
                op.finish = s + op.dur
                eng_free[op.eng] = op.finish
            order[op.eng].append(op)
            for su in op.succs:
                su.npred -= 1
                lat = 0.05 if (su.eng == op.eng and not op.dma) else 0.15
                if op.finish + lat > su.ready:
                    su.ready = op.finish + lat
                if su.npred == 0:
                    ready[su.eng].append(su)
            remaining -= 1
        self.order = order
        self.makespan = max(op.finish for op in self.ops)
        self.busy = {e: sum((o.dur if not o.dma else 0.06) for o in order[e]) for e in ENGS}
        for e in ENGS:
            cnt = 0
            for op in order[e]:
                if op.need_sig and not op.dma:
                    cnt += 1
                    op.sigval = cnt
        return order

    def emit(self, eng, handle, engsem):
        waited = {}
        for op in self.order[eng]:
            toks = {}
            for p, sync in op.preds.items():
                if not sync:
                    continue
                if p.dma:
                    sem, val = p.dma_sem, p.dma_val
                else:
                    sem, val = engsem[p.eng], p.sigval
                k = id(sem)
                if k not in toks or toks[k][1] < val:
                    toks[k] = (sem, val)
            for k, (sem, val) in toks.items():
                if waited.get(k, 0) < val:
                    handle.wait_ge(sem, val)
                    waited[k] = val
            inst = op.emit(handle)
            if op.dma:
                inst.then_inc(op.dma_sem, 16)
            elif op.need_sig:
                inst.then_inc(engsem[eng], 1)


def build_program(nseq, depth):
    ntok = nseq * 2048
    npass = nseq * 2
    nc = bass.Bass("TRN2", target_bir_lowering=False)
    xT = nc.dram_tensor("xT", [D, ntok], F32, kind="ExternalInput").ap()
    wA = nc.dram_tensor("wA", [depth * 160, 128, 2048], F32, kind="ExternalInput").ap()
    wD = nc.dram_tensor("wD", [depth * 64, 128, 1408], F32, kind="ExternalInput").ap()
    prm_d = nc.dram_tensor("prm", [128, NPRM], F32, kind="ExternalInput").ap()
    cst_d = nc.dram_tensor("cst", [128, 1152], F32, kind="ExternalInput").ap()
    yT = nc.dram_tensor("yT", [D, ntok], F32, kind="ExternalOutput").ap()
    sst = nc.dram_tensor("sst", [depth * 8, 128, 128], F32, kind="Internal").ap()

    S = Sched()
    st = contextlib.ExitStack()
    with st:
        def sb(name, shape, dt):
            return st.enter_context(nc.sbuf_tensor(name, shape, dt))

        xres = sb("xres", [128, KC, TP], F32)
        hT = sb("hT", [128, KC, TP], BF16)
        big = sb("big", [128, KC, TP], BF16)
        wbf = sb("wbf", [128, NBF, 2048], BF16)
        Ft = sb("Ft", [128, NF, FW], F32)
        Ht = sb("Ht", [128, NH, TT], BF16)
        vtok = sb("vtok", [64, 1024], BF16)
        khtok = sb("khtok", [64, 1024], BF16)
        ptb = sb("ptb", [64, TT], BF16)
        sbb = sb("sbb", [128, 8, 128], BF16)
        s32 = sb("s32", [128, 9, 128], F32)
        sm = sb("sm", [128, 32], F32)
        prm = sb("prm_sb", [128, NPRM], F32)
        lbw = sb("lbw", [128, 128], F32)
        rmask = sb("rmask", [128, TT], F32)
        cmask = sb("cmask", [64, TT], BF16)
        ident = sb("ident", [128, 128], BF16)
        ones = sb("ones", [128, 128], BF16)
        cvals = sb("cvals", [128, 4], F32)
        utail = sb("utail", [128, LFULL * 8 * 2], F32)
        pbs = [st.enter_context(nc.psum_tensor("pb%d" % k, [128, TT], F32)) for k in range(8)]

        engsem = {e: st.enter_context(nc.semaphore("es_" + e)) for e in ENGS}
        wsem = [st.enter_context(nc.semaphore("ws%d" % i)) for i in range(NBF)]
        xsem = [st.enter_context(nc.semaphore("xs%d" % i)) for i in range(KC)]
        osem = [st.enter_context(nc.semaphore("os%d" % i)) for i in range(4)]
        csem = [st.enter_context(nc.semaphore("cs%d" % i)) for i in range(4)]
        sldsem = st.enter_context(nc.semaphore("sld"))
        sstsem = st.enter_context(nc.semaphore("sst"))

        X = [[Buf("x%d_%d" % (i, t)) for t in range(2)] for i in range(KC)]
        HT = [Buf("hT%d" % t) for t in range(2)]
        BIG = [[Buf("big%d_%d" % (i, t)) for t in range(2)] for i in range(KC)]
        WBF = [Buf("wbf%d" % i) for i in range(NBF)]
        PB = [Buf("pb%d" % i) for i in range(8)]
        FB = [Buf("f%d" % i) for i in range(NF)]
        HB = [Buf("h%d" % i) for i in range(NH)]
        VTOK = Buf("vtok")
        KHTOK = Buf("khtok")
        PTB = Buf("ptb")
        SBB = Buf("sbb")
        S32B = [Buf("s32_%d" % i) for i in range(9)]
        SM = [Buf("sm%d" % i) for i in range(4)]
        UT = [[Buf("ut%d_%d" % (l, g)) for g in range(8)] for l in range(LFULL)]
        STD = [[Buf("std%d_%d" % (l, h)) for h in range(8)] for l in range(depth)]
        CONST = Buf("const")
        PRM = Buf("prm")
        LBW = Buf("lbw")

        def F(k, w=TT, o=0):
            return Ft[:, k, o:o + w]

        def H(k):
            return Ht[:, k, :]

        def pcol(c):
            return prm[:, c:c + 1]

        def act_dur(n):
            return 0.22 + n * 0.00083

        def dve_dur(n, two_sbuf=False):
            return 0.22 + n * 0.00104

        def ACT(out, in_, func, reads, writes, scale=None, bias=None, n=TT):
            kw = {}
            if scale is not None:
                kw["scale"] = scale
            if bias is not None:
                kw["bias"] = bias
            S.add("act", lambda e: e.activation(out=out, in_=in_, func=func, **kw), reads, writes, dur=act_dur(n))

        def TTENS(out, in0, in1, op, reads, writes, n=TT, two_sbuf=False, eng="dve"):
            S.add(eng, lambda e: e.tensor_tensor(out=out, in0=in0, in1=in1, op=op), reads, writes,
                  dur=dve_dur(n, two_sbuf))

        def STT(out, in0, scalar, in1, op0, op1, reads, writes, n=TT, two_sbuf=False):
            S.add("dve", lambda e: e.scalar_tensor_tensor(out=out, in0=in0, scalar=scalar, in1=in1, op0=op0, op1=op1),
                  reads, writes, dur=dve_dur(n, two_sbuf))

        def TSC(out, in0, s1, s2, op0, op1, reads, writes, n=TT, eng="dve"):
            if op1 is None:
                S.add(eng, lambda e: e.tensor_scalar(out=out, in0=in0, scalar1=s1, scalar2=None, op0=op0),
                      reads, writes, dur=dve_dur(n))
            else:
                S.add(eng, lambda e: e.tensor_scalar(out=out, in0=in0, scalar1=s1, scalar2=s2, op0=op0, op1=op1),
                      reads, writes, dur=dve_dur(n))

        def COPY(eng, out, in_, reads, writes, n=TT):
            S.add(eng, lambda e: e.tensor_copy(out=out, in_=in_), reads, writes, dur=dve_dur(n))

        def MMG(out_ap, lhs_list, rhs_list, reads, writes, n=TT):
            k = len(lhs_list)

            def emit(e):
                inst = None
                for i in range(k):
                    inst = e.matmul(out_ap, lhs_list[i], rhs_list[i], start=(i == 0), stop=(i == k - 1))
                return inst
            S.add("pe", emit, reads, writes, dur=k * (max(n, 64) / 1950.0 + 0.002))

        def MM1(out_ap, lhs, rhs, start, stop, reads, writes, n=TT):
            S.add("pe", lambda e: e.matmul(out_ap, lhs, rhs, start=start, stop=stop), reads, writes,
                  dur=max(n, 64) / 1950.0 + 0.03)

        wctr = [0]

        def wget(src_ap, width):
            n = wctr[0]
            wctr[0] += 1
            s2 = n % NBF
            S.add("pool", lambda e: e.dma_start(out=wbf[:, s2, 0:width], in_=src_ap), [], [WBF[s2]],
                  dma_sem=wsem[s2], dma_bytes=128 * width * 4)
            return wbf[:, s2, :], WBF[s2]

        S.add("sp", lambda e: e.dma_start(out=prm[:], in_=prm_d), [], [PRM], dma_sem=csem[0], dma_bytes=128 * NPRM * 4)
        S.add("sp", lambda e: e.dma_start(out=rmask[:], in_=cst_d[:, 0:512]), [], [CONST], dma_sem=csem[1], dma_bytes=2 ** 18)
        S.add("sp", lambda e: e.dma_start(out=Ft[0:64, 0, 0:512], in_=cst_d[0:64, 512:1024]), [], [FB[0]], dma_sem=csem[2], dma_bytes=2 ** 17)
        S.add("sp", lambda e: e.dma_start(out=Ft[:, 1, 0:128], in_=cst_d[:, 1024:1152]), [], [FB[1]], dma_sem=csem[3], dma_bytes=2 ** 16)
        COPY("pool", cmask[:], Ft[0:64, 0, 0:512], [FB[0]], [CONST])
        COPY("pool", ident[:], Ft[:, 1, 0:128], [FB[1]], [CONST])
        S.add("pool", lambda e: e.memset(ones[:], 1.0), [], [CONST], dur=0.2)
        S.add("pool", lambda e: e.memset(cvals[:, 0:1], EPS), [], [CONST], dur=0.1)
        S.add("pool", lambda e: e.memset(cvals[:, 1:2], 1.0), [], [CONST], dur=0.1)
        eps_ap = cvals[:, 0:1]
        one_ap = cvals[:, 1:2]
        LBL = 304

        def lsl(l):
            return prm[:, LBL + l:LBL + 32:4]

        w = lambda a, b: lbw[:, a:b]
        TTENS(w(96, 104), lsl(0), lsl(1), ALU.max, [PRM], [LBW], n=8)
        TTENS(w(104, 112), lsl(2), lsl(3), ALU.max, [PRM], [LBW], n=8)
        TTENS(w(96, 104), w(96, 104), w(104, 112), ALU.max, [LBW], [LBW], n=8)
        for l in range(4):
            TTENS(w(l * 8, l * 8 + 8), lsl(l), w(96, 104), ALU.subtract, [PRM, LBW], [LBW], n=8)
        ACT(w(0, 32), w(0, 32), AF.Exp, [LBW, CONST], [LBW], n=32)
        TTENS(w(104, 112), w(0, 8), w(8, 16), ALU.add, [LBW], [LBW], n=8)
        TTENS(w(112, 120), w(16, 24), w(24, 32), ALU.add, [LBW], [LBW], n=8)
        TTENS(w(104, 112), w(104, 112), w(112, 120), ALU.add, [LBW], [LBW], n=8)
        S.add("dve", lambda e: e.reciprocal(out=w(104, 112), in_=w(104, 112)), [LBW], [LBW], dur=0.2)
        for l in range(1, 4):
            TTENS(w(l * 8, l * 8 + 8), w(l * 8, l * 8 + 8), w(104, 112), ALU.mult, [LBW], [LBW], n=8)
        TTENS(w(0, 8), w(0, 8), w(0, 8), ALU.subtract, [LBW], [LBW], n=8)
        TTENS(w(16, 24), w(8, 16), w(16, 24), ALU.add, [LBW], [LBW], n=8)
        TTENS(w(24, 32), w(16, 24), w(24, 32), ALU.add, [LBW], [LBW], n=8)
        TSC(w(0, 32), w(0, 32), 0.0, 1.0 - 1e-4, ALU.max, ALU.min, [LBW], [LBW], n=32)
        TSC(w(32, 64), w(0, 32), -1.0, 1.0, ALU.mult, ALU.add, [LBW], [LBW], n=32)
        ACT(w(64, 96), w(32, 64), AF.Ln, [LBW], [LBW], n=32)

        def lb_ap(l, hd):
            return lbw[:, l * 8 + hd:l * 8 + hd + 1]

        def lnoml_ap(l, hd):
            return lbw[:, 64 + l * 8 + hd:64 + l * 8 + hd + 1]

        sq_rr = [0]
        ft_rr = [0]

        def tsl_(tt):
            return slice(tt * TT, (tt + 1) * TT)

        def stat_accumulate(i, tt, first, last):
            k = sq_rr[0] % 4
            sq_rr[0] += 1
            ACT(H(k), xres[:, i, tsl_(tt)], AF.Square, [X[i][tt], CONST], [HB[k]])
            MM1(pbs[6 + tt][:], ones[:], H(k), first, last, [HB[k], CONST], [PB[6 + tt]])

        def rstd_to_psum(tt):
            ACT(F(8 + tt), pbs[6 + tt][:], AF.Ln, [PB[6 + tt], CONST], [FB[8 + tt]], scale=1.0 / D, bias=eps_ap)
            ACT(pbs[6 + tt][:], F(8 + tt), AF.Exp, [FB[8 + tt]], [PB[6 + tt]], scale=-0.5)

        def norm_finalize(gbase):
            for tt in range(2):
                rstd_to_psum(tt)
                for kc in range(KC):
                    STT(hT[:, kc, tsl_(tt)], xres[:, kc, tsl_(tt)], pcol(gbase + kc), pbs[6 + tt][:], ALU.mult, ALU.mult,
                        [X[kc][tt], PB[6 + tt], PRM], [HT[tt]])

        pair_rr = [0]

        def next_pair(npairs=2):
            p = pair_rr[0] % npairs
            pair_rr[0] += 1
            return 2 * p

        def pair_group(b0, wap, rhs, nk, reads):
            lhs = [wap[:, k * 128:(k + 1) * 128] for k in range(nk)]

            def emit(e):
                inst = None
                for k in range(nk):
                    e.matmul(pbs[b0][:], lhs[k], rhs(k, 0), start=(k == 0), stop=(k == nk - 1))
                    inst = e.matmul(pbs[b0 + 1][:], lhs[k], rhs(k, 1), start=(k == 0), stop=(k == nk - 1))
                return inst
            S.add("pe", emit, reads, [PB[b0], PB[b0 + 1]], dur=2 * nk * (TT / 2400.0 + 0.003))

        def h_rhs(k, tt):
            return hT[:, k, tsl_(tt)]

        def big_rhs(k, tt):
            return big[:, k, tsl_(tt)]

        def in_proj(blk_idx):
            wap, wbuf = wget(wA[blk_idx], 2048)
            b0 = next_pair()
            pair_group(b0, wap, h_rhs, KC, [wbuf, HT[0], HT[1]])
            return b0

        def bigf(m):
            return big[:, m, :].bitcast(F32), [BIG[m][0], BIG[m][1]]

        def early_tiles(hd, tt):
            if tt == 1 and hd % 2 == 1 and hd < 7:
                return [bigf(6), bigf(7), bigf(14), bigf(15)]
            if tt == 1 and hd == 6:
                return [bigf(7), bigf(15), (F(6), [FB[6]]), (F(7), [FB[7]])]
            return [(F(4 * tt + k), [FB[4 * tt + k]]) for k in range(4)]

        def hgrn_chain(l, hd, tt):
            lb = lb_ap(l, hd)
            lnoml = lnoml_ap(l, hd)
            gn = pcol(272 + l * 8 + hd)
            (Tz, Bz), (Tt, Bt), (Tq, Bq), (Tg, Bg) = early_tiles(hd, tt)
            ivb = 0 if tt == 0 else 6
            ACT(F(10), Tz, AF.Ln, Bz + [CONST], [FB[10]], bias=one_ap)
            ACT(F(8), Tz, AF.Ln, Bz + [LBW], [FB[8]], scale=lb, bias=one_ap)
            ACT(Tt, Tt, AF.Ln, Bt, Bt, bias=one_ap)
            TTENS(F(8), F(8), F(10), ALU.subtract, [FB[8], FB[10]], [FB[8]], two_sbuf=True)
            S.add("dve", lambda e: e.tensor_tensor_scan(out=Tz, data0=rmask[:], data1=F(8), initial=0.0,
                                                       op0=ALU.mult, op1=ALU.add),
                  [FB[8], CONST], Bz, dur=dve_dur(TT, True))
            b3 = Tz.rearrange("p (c t) -> p c t", t=64)
            bp3 = F(8).rearrange("p (c t) -> p c t", t=64)
            TTENS(bp3, b3, b3[:, :, 31:32].broadcast_to([128, 8, 64]), ALU.subtract, Bz, [FB[8]],
                  two_sbuf=True)
            TTENS(Tt, Tt, F(8), ALU.add, Bt + [FB[8]], Bt, two_sbuf=True)
            ACT(H(1), Tt, AF.Exp, Bt + [LBW], [HB[1]], scale=-1.0, bias=lnoml)
            ACT(F(9), Tq, AF.Exp, Bq, [FB[9]], scale=-1.0)
            ACT(F(9), F(9), AF.Ln, [FB[9]], [FB[9]], bias=one_ap)
            ACT(F(9), F(9), AF.Exp, [FB[9]], [FB[9]], scale=-1.0)
            TTENS(Tq, Tq, F(9), ALU.mult, Bq + [FB[9]], Bq, two_sbuf=True)
            ACT(F(9), F(8), AF.Exp, [FB[8]] + Bq, [FB[9]])
            TTENS(H(2), Tq, F(9), ALU.mult, Bq + [FB[9]], [HB[2]], two_sbuf=True)
            ACT(F(11), Tg, AF.Exp, Bg, [FB[11]], scale=-1.0)
            ACT(F(11), F(11), AF.Ln, [FB[11]], [FB[11]], bias=one_ap)
            ACT(F(11), F(11), AF.Exp, [FB[11]], [FB[11]], scale=-1.0)
            TTENS(Tg, Tg, F(11), ALU.mult, Bg + [FB[11]], Bg, two_sbuf=True)
            ACT(sm[:, 0:8], b3[:, :, 31], AF.Exp, Bz, [SM[0]], n=8)
            ACT(sm[:, 8:16], b3[:, :, 63], AF.Exp, Bz, [SM[1]], n=8)
            TTENS(sm[:, 16:24], b3[:, :, 63], b3[:, :, 31], ALU.subtract, Bz, [SM[2]], n=8)
            ACT(sm[:, 24:32], sm[:, 16:24], AF.Exp, [SM[2]], [SM[3]], n=8)
            q3 = H(2).rearrange("p (c t) -> p c t", t=64)
            k3 = H(1).rearrange("p (c t) -> p c t", t=64)
            qh3 = H(3).rearrange("p (c t) -> p c t", t=64)
            kh3 = H(4).rearrange("p (c t) -> p c t", t=64)
            TTENS(qh3, q3, sm[:, 0:8].unsqueeze(2).broadcast_to([128, 8, 64]), ALU.mult, [HB[2], SM[0]], [HB[3]],
                  two_sbuf=True)
            TTENS(kh3, k3, sm[:, 24:32].unsqueeze(2).broadcast_to([128, 8, 64]), ALU.mult, [HB[1], SM[3]], [HB[4]],
                  two_sbuf=True)
            p4b = pbs[4][:].bitcast(BF16)
            p5b = pbs[5][:].bitcast(BF16)

            def tr_emit(dst, src):
                def emit(e):
                    inst = None
                    for c in range(8):
                        inst = e.transpose(out=dst[0:64, c * 128:(c + 1) * 128], in_=src[:, c * 64:(c + 1) * 64],
                                           identity=ident[:])
                    return inst
                return emit
            S.add("pe", tr_emit(p4b, H(ivb)), [HB[ivb], CONST], [PB[4]], dur=0.8)
            ACT(vtok[:], p4b[0:64, :], AF.Copy, [PB[4]], [VTOK], n=1024)
            S.add("pe", tr_emit(p5b, H(4)), [HB[4], CONST], [PB[5]], dur=0.8)
            COPY("dve", khtok[:], p5b[0:64, :], [PB[5]], [KHTOK], n=1024)

            def sc_emit(e):
                inst = None
                for c in range(8):
                    cs = slice(c * 64, (c + 1) * 64)
                    inst = e.matmul(pbs[6][0:64, cs], Ht[:, 1, cs], Ht[:, 2, cs], start=True, stop=True)
                return inst
            S.add("pe", sc_emit, [HB[1], HB[2]], [PB[6]], dur=0.6)
            TTENS(ptb[:], pbs[6][0:64, :], cmask[:], ALU.mult, [PB[6], CONST], [PTB])

            def ds_emit(bank, c0):
                def emit(e):
                    inst = None
                    for c in range(c0, c0 + 4):
                        inst = e.matmul(pbs[bank][:, (c - c0) * 128:(c - c0 + 1) * 128],
                                        khtok[:, c * 128:(c + 1) * 128], vtok[:, c * 128:(c + 1) * 128],
                                        start=True, stop=True)
                    return inst
                return emit
            S.add("pe", ds_emit(7, 0), [KHTOK, VTOK], [PB[7]], dur=0.5)
            S.add("pe", ds_emit(4, 4), [KHTOK, VTOK], [PB[4]], dur=0.5)
            for c in range(8):
                bank = 7 if c < 4 else 4
                cc = c % 4
                STT(s32[:, c + 1, :], s32[:, c, :], sm[:, 8 + c:9 + c], pbs[bank][:, cc * 128:(cc + 1) * 128],
                    ALU.mult, ALU.add, [S32B[c], SM[1], PB[bank]], [S32B[c + 1]], n=128)
            ACT(sbb[:], s32[:, 0:8, :], AF.Copy, [S32B[c] for c in range(8)], [SBB], n=1024)

            def o_emit(e):
                inst = None
                for c in range(8):
                    cs = slice(c * 64, (c + 1) * 64)
                    e.matmul(pbs[5][:, cs], vtok[:, c * 128:(c + 1) * 128], ptb[:, cs], start=True, stop=False)
                    inst = e.matmul(pbs[5][:, cs], sbb[:, c, :], Ht[:, 3, cs], start=False, stop=True)
                return inst
            S.add("pe", o_emit, [VTOK, PTB, SBB, HB[3]], [PB[5]], dur=1.2)
            ACT(H(5), pbs[5][:], AF.Square, [PB[5]], [HB[5]])
            MM1(pbs[6][:], ones[:], H(5), True, True, [HB[5], CONST], [PB[6]])
            ACT(F(10), pbs[6][:], AF.Ln, [PB[6], CONST], [FB[10]], scale=1.0 / 128, bias=eps_ap)
            ACT(F(10), F(10), AF.Exp, [FB[10]], [FB[10]], scale=-0.5)
            STT(F(10), pbs[5][:], gn, F(10), ALU.mult, ALU.mult, [PB[5], FB[10], PRM], [FB[10]])
            TTENS(big[:, 8 + hd, tsl_(tt)], F(10), Tg, ALU.mult, [FB[10]] + Bg, [BIG[8 + hd][tt]], two_sbuf=True)
            if tt == 0:
                COPY("dve", s32[:, 0, :], s32[:, 8, :], [S32B[8]], [S32B[0]], n=128)

        def hgrn_unit(l, hd, half):
            base = l * 160 + hd * 7
            if half == 0:
                S.add("dve", lambda e: e.memset(s32[:, 0, :], 0.0), [], [S32B[0]], dur=0.2)
            else:
                S.add("sp", lambda e: e.dma_start(out=s32[:, 0, :], in_=sst[l * 8 + hd]), [STD[l][hd]], [S32B[0]],
                      dma_sem=sldsem, dma_bytes=2 ** 16)
            et = [early_tiles(hd, 0), early_tiles(hd, 1)]
            b0 = in_proj(base + 0)
            for tt in range(2):
                ACT(et[tt][2][0], pbs[b0 + tt][:], AF.Copy, [PB[b0 + tt]], et[tt][2][1])
            b0 = in_proj(base + 1)
            for tt in range(2):
                ACT(et[tt][0][0], pbs[b0 + tt][:], AF.Exp, [PB[b0 + tt], CONST], et[tt][0][1], scale=-1.0)
                ACT(et[tt][1][0], pbs[b0 + tt][:], AF.Exp, [PB[b0 + tt]], et[tt][1][1])
            b0 = in_proj(base + 2)
            for tt in range(2):
                ivb = 0 if tt == 0 else 6
                ACT(H(ivb), pbs[b0 + tt][:], AF.Copy, [PB[b0 + tt]], [HB[ivb]])
            b0 = in_proj(base + 3)
            for tt in range(2):
                ACT(et[tt][3][0], pbs[b0 + tt][:], AF.Copy, [PB[b0 + tt]], et[tt][3][1])
            for tt in range(2):
                hgrn_chain(l, hd, tt)
            if half == 0:
                S.add("sp", lambda e: e.dma_start(out=sst[l * 8 + hd], in_=s32[:, 8, :]), [S32B[8]], [STD[l][hd]],
                      dma_sem=sstsem, dma_bytes=2 ** 16)

        def conv_unit(l, g):
            base = l * 160 + g * 7 + 4
            cw = 144 + (l * 8 + g) * 3
            gn = pcol(240 + l * 8 + g)
            ut = utail[:, (l * 8 + g) * 2:(l * 8 + g) * 2 + 2]
            IA = (12, 13)
            IU = (14, 15)
            b0 = in_proj(base + 2)
            for tt in range(2):
                ACT(F(IA[tt]), pbs[b0 + tt][:], AF.Copy, [PB[b0 + tt]], [FB[IA[tt]]])
            b0 = in_proj(base + 1)
            for tt in range(2):
                iu = IU[tt]
                if tt == 0:
                    ACT(Ft[:, iu, 0:2], ut, AF.Copy, [UT[l][g]], [FB[iu]], n=2)
                TTENS(Ft[:, iu, 2:514], pbs[b0 + tt][:], F(IA[tt]), ALU.mult, [PB[b0 + tt], FB[IA[tt]]], [FB[iu]])
                if tt == 0:
                    ACT(Ft[:, IU[1], 0:2], Ft[:, IU[0], 512:514], AF.Copy, [FB[IU[0]]], [FB[IU[1]]], n=2)
                else:
                    ACT(ut, Ft[:, IU[1], 512:514], AF.Copy, [FB[IU[1]]], [UT[l][g]], n=2)
            b0 = in_proj(base + 0)
            for tt in range(2):
                iu, ia = IU[tt], IA[tt]
                TSC(F(ia), Ft[:, iu, 0:512], pcol(cw), None, ALU.mult, None, [FB[iu], PRM], [FB[ia]])
                STT(F(ia), Ft[:, iu, 1:513], pcol(cw + 1), F(ia), ALU.mult, ALU.add, [FB[iu], FB[ia], PRM], [FB[ia]])
                STT(F(ia), Ft[:, iu, 2:514], pcol(cw + 2), F(ia), ALU.mult, ALU.add, [FB[iu], FB[ia], PRM], [FB[ia]])
                TTENS(F(ia), pbs[b0 + tt][:], F(ia), ALU.mult, [PB[b0 + tt], FB[ia]], [FB[ia]])
                ACT(H(5), F(ia), AF.Square, [FB[ia]], [HB[5]])
                MM1(pbs[6][:], ones[:], H(5), True, True, [HB[5], CONST], [PB[6]])
                ACT(F(iu), pbs[6][:], AF.Ln, [PB[6], CONST], [FB[iu]], scale=1.0 / 128, bias=eps_ap)
                ACT(F(iu), F(iu), AF.Exp, [FB[iu]], [FB[iu]], scale=-0.5)
                STT(big[:, g, tsl_(tt)], F(ia), gn, F(iu), ALU.mult, ALU.mult, [FB[ia], FB[iu], PRM], [BIG[g][tt]])

        def out_proj(l):
            base = l * 160 + 56
            for i in range(KC):
                wap, wbuf = wget(wA[base + i], 2048)
                b0 = next_pair()
                pair_group(b0, wap, big_rhs, KC, [wbuf] + [BIG[kc][tt] for kc in range(KC) for tt in range(2)])
                for tt in range(2):
                    TTENS(xres[:, i, tsl_(tt)], pbs[b0 + tt][:], xres[:, i, tsl_(tt)], ALU.add, [PB[b0 + tt], X[i][tt]], [X[i][tt]])
                    stat_accumulate(i, tt, i == 0, i == KC - 1)

        def ffn(l):
            gbase = l * 160 + 72
            for qd in range(NQ):
                for j in range(JQ):
                    jj = qd * JQ + j
                    wg = wget(wA[gbase + 2 * jj], 2048)
                    bg = next_pair(4)
                    pair_group(bg, wg[0], h_rhs, KC, [wg[1], HT[0], HT[1]])
                    wu = wget(wA[gbase + 2 * jj + 1], 2048)
                    bu = next_pair(4)
                    pair_group(bu, wu[0], h_rhs, KC, [wu[1], HT[0], HT[1]])
                    for tt in range(2):
                        k = ft_rr[0] % 4
                        ft_rr[0] += 1
                        ACT(F(k), pbs[bg + tt][:], AF.Silu, [PB[bg + tt]], [FB[k]])
                        TTENS(big[:, j, tsl_(tt)], pbs[bu + tt][:], F(k), ALU.mult, [PB[bu + tt], FB[k]], [BIG[j][tt]])
                for i in range(KC):
                    wap, wbuf = wget(wD[l * 64 + qd * 16 + i], 1408)
                    b0 = next_pair(3)
                    pair_group(b0, wap, big_rhs, JQ, [wbuf] + [BIG[j][tt] for j in range(JQ) for tt in range(2)])
                    for tt in range(2):
                        TTENS(xres[:, i, tsl_(tt)], pbs[b0 + tt][:], xres[:, i, tsl_(tt)], ALU.add, [PB[b0 + tt], X[i][tt]], [X[i][tt]])
                        if qd == NQ - 1:
                            stat_accumulate(i, tt, i == 0, i == KC - 1)

        for ps in range(npass):
            seq, half = divmod(ps, 2)
            tok0 = seq * 2048 + half * TP
            for i in range(KC):
                S.add("sp", lambda e, i=i, tok0=tok0: e.dma_start(out=xres[:, i, :], in_=xT[i * 128:(i + 1) * 128, tok0:tok0 + TP]),
                      [], [X[i][0], X[i][1]], dma_sem=xsem[i], dma_bytes=128 * TP * 4)
            if half == 0:
                S.add("dve", lambda e: e.memset(utail[:], 0.0), [], [UT[l][g] for l in range(LFULL) for g in range(8)], dur=0.2)
            for tt in range(2):
                for i in range(KC):
                    stat_accumulate(i, tt, i == 0, i == KC - 1)
            for l in range(depth):
                norm_finalize(l * 16)
                for u in range(8):
                    hgrn_unit(l, u, half)
                    conv_unit(l, u)
                out_proj(l)
                norm_finalize(64 + l * 16)
                ffn(l)
            for tt in range(2):
                rstd_to_psum(tt)
                for i in range(KC):
                    k = ft_rr[0] % 4
                    ft_rr[0] += 1
                    STT(F(k), xres[:, i, tsl_(tt)], pcol(128 + i), pbs[6 + tt][:], ALU.mult, ALU.mult,
                        [X[i][tt], PB[6 + tt], PRM], [FB[k]])
                    S.add("sp", lambda e, i=i, k=k, t0=tok0 + tt * TT: e.dma_start(out=yT[i * 128:(i + 1) * 128, t0:t0 + TT], in_=F(k)),
                          [FB[k]], [], dma_sem=osem[k], dma_bytes=128 * TT * 4)

        S.schedule()
        build_program.last_sched = S
        final_waits = [(osem[k], S.dma_counts.get(id(osem[k]), 0) * 16) for k in range(4)]

        with nc.Block() as block:
            @block.tensor
            def _(e):
                S.emit("pe", e, engsem)

            @block.scalar
            def _(e):
                S.emit("act", e, engsem)

            @block.vector
            def _(e):
                S.emit("dve", e, engsem)

            @block.gpsimd
            def _(e):
                S.emit("pool", e, engsem)

            @block.sync
            def _(e):
                S.emit("sp", e, engsem)
                for sem, val in final_waits:
                    if val:
                        e.wait_ge(sem, val)
    return nc


def _prep_weights(w_in, w_out, w_gate, w_up, w_down, depth):
    wA = np.empty((depth * 160, 128, 2048), np.float32)
    wD = np.empty((depth * 64, 128, 1408), np.float32)
    order = []
    for u in range(8):
        order += [24 + u, 32 + u, 40 + u, 48 + u, u, 8 + u, 16 + u]
    for l in range(depth):
        wi = np.asarray(w_in[l]).reshape(16, 128, 56, 128).transpose(2, 1, 0, 3).reshape(56, 128, 2048)
        wA[l * 160:l * 160 + 56] = wi[order]
        wA[l * 160 + 56:l * 160 + 72] = np.asarray(w_out[l]).reshape(16, 128, 16, 128).transpose(2, 1, 0, 3).reshape(16, 128, 2048)
        wg = np.asarray(w_gate[l]).reshape(16, 128, 44, 128).transpose(2, 1, 0, 3).reshape(44, 128, 2048)
        wu = np.asarray(w_up[l]).reshape(16, 128, 44, 128).transpose(2, 1, 0, 3).reshape(44, 128, 2048)
        wA[l * 160 + 72:l * 160 + 160:2] = wg
        wA[l * 160 + 73:l * 160 + 160:2] = wu
        wD[l * 64:(l + 1) * 64] = np.asarray(w_down[l]).reshape(4, 11, 128, 16, 128).transpose(0, 3, 2, 1, 4).reshape(64, 128, 1408)
    return wA, wD


def _prep_params(norm_mix, conv_w, gn_conv, lb_logits, gn_hgrn, norm_ffn, norm_final):
    prm = np.zeros((128, NPRM), np.float32)
    L = norm_mix.shape[0]
    for l in range(L):
        prm[:, l * 16:(l + 1) * 16] = np.asarray(norm_mix[l]).reshape(16, 128).T
        prm[:, 64 + l * 16:64 + (l + 1) * 16] = np.asarray(norm_ffn[l]).reshape(16, 128).T
        prm[:, 144 + l * 24:144 + (l + 1) * 24] = np.asarray(conv_w[l]).reshape(3, 8, 128).transpose(2, 1, 0).reshape(128, 24)
        prm[:, 240 + l * 8:240 + (l + 1) * 8] = np.asarray(gn_conv[l]).reshape(8, 128).T
        prm[:, 272 + l * 8:272 + (l + 1) * 8] = np.asarray(gn_hgrn[l]).reshape(8, 128).T
    prm[:, 128:144] = np.asarray(norm_final).reshape(16, 128).T
    prm[:, 304:336] = np.asarray(lb_logits).reshape(4, 8, 128).transpose(2, 1, 0).reshape(128, 32)
    return prm


def _consts():
    c = np.zeros((128, 1152), np.float32)
    c[:, 0:512] = 1.0
    c[:, 0:512:64] = 0.0
    cm = (np.arange(64)[:, None] <= np.arange(64)[None, :]).astype(np.float32)
    c[0:64, 512:1024] = np.tile(cm, (1, 8))
    c[:, 1024:1152] = np.eye(128, dtype=np.float32)
    return c


def kernel_impl(inputs, ncores, nseq, depth):
    x = np.asarray(inputs["x"], dtype=np.float32)
    wA, wD = _prep_weights(inputs["w_in"], inputs["w_out"], inputs["w_gate"], inputs["w_up"], inputs["w_down"], depth)
    prm = _prep_params(np.asarray(inputs["norm_mix"]), np.asarray(inputs["conv_w"]), np.asarray(inputs["gn_conv"]),
                       np.asarray(inputs["lb_logits"]), np.asarray(inputs["gn_hgrn"]), np.asarray(inputs["norm_ffn"]),
                       np.asarray(inputs["norm_final"]))
    cst = _consts()
    nc = build_program(nseq, depth)
    in_maps = []
    for c in range(ncores):
        xc = x[c * nseq:(c + 1) * nseq].reshape(nseq * 2048, D)
        in_maps.append({"xT": np.ascontiguousarray(xc.T), "wA": wA, "wD": wD, "prm": prm, "cst": cst})
    res = run_bass_kernel_spmd(nc, in_maps, core_ids=list(range(ncores)))
    out = np.empty((ncores * nseq, 2048, D), np.float32)
    for c in range(ncores):
        yT = res.results[c]["yT"]
        out[c * nseq:(c + 1) * nseq] = np.ascontiguousarray(yT.T).reshape(nseq, 2048, D)
    return out


def kernel(**inputs):
    return kernel_impl(inputs, 8, 2, 4)
```

```python
import contextlib
import numpy as np
import concourse.bass as bass
import concourse.mybir as mybir
from concourse.bass_utils import run_bass_kernel_spmd

F32 = mybir.dt.float32
BF16 = mybir.dt.bfloat16
AF = mybir.ActivationFunctionType
ALU = mybir.AluOpType

D = 2048
KC = 16
NJ = 44
NQ = 4
JQ = 11
TT = 512
TP = 1024
LFULL = 4
EPS = 1e-6
NST = 2
NBF = 5
NF = 16
NH = 7
FW = 520
NPRM = 336

ENGS = ("pe", "act", "dve", "pool", "sp")
DMA_BW = 170e3
DMA_LAT = 2.0
PE_SLACK = 0.0


class Buf:
    __slots__ = ("name", "lw", "rd")

    def __init__(self, name):
        self.name = name
        self.lw = None
        self.rd = []


class Op:
    __slots__ = ("idx", "eng", "emit", "dur", "preds", "succs", "npred", "ready", "start",
                 "finish", "dma", "dma_sem", "dma_val", "dma_bytes", "need_sig", "sigval")


class Sched:
    def __init__(self):
        self.ops = []
        self.dma_counts = {}

    def add(self, eng, emit, reads=(), writes=(), dur=0.3, dma_sem=None, dma_bytes=0):
        op = Op()
        op.idx = len(self.ops)
        op.eng = eng
        op.emit = emit
        op.dur = dur
        op.preds = {}
        op.succs = []
        op.ready = 0.0
        op.need_sig = False
        op.sigval = 0
        op.dma = dma_sem is not None
        op.dma_sem = dma_sem
        op.dma_bytes = dma_bytes
        if op.dma:
            k = self.dma_counts.get(id(dma_sem), 0) + 1
            self.dma_counts[id(dma_sem)] = k
            op.dma_val = 16 * k
        else:
            op.dma_val = 0
        for b in reads:
            if b.lw is not None:
                self._dep(op, b.lw, "raw")
        for b in writes:
            if b.lw is not None:
                self._dep(op, b.lw, "waw")
            for r in b.rd:
                if r is not op:
                    self._dep(op, r, "war")
        for b in reads:
            b.rd.append(op)
        for b in writes:
            b.lw = op
            b.rd = []
        op.npred = len(op.preds)
        self.ops.append(op)
        return op

    def _dep(self, op, p, kind):
        if p is op:
            return
        if p.dma:
            sync = True
        elif p.eng != op.eng:
            sync = True
        else:
            sync = p.eng != "pe"
        if sync:
            p.need_sig = True
        if p in op.preds:
            op.preds[p] = op.preds[p] or sync
        else:
            op.preds[p] = sync
            p.succs.append(op)

    def schedule(self):
        ready = {e: [] for e in ENGS}
        eng_free = {e: 0.0 for e in ENGS}
        dma_free = 0.0
        order = {e: [] for e in ENGS}
        for op in self.ops:
            if op.npred == 0:
                ready[op.eng].append(op)
        remaining = len(self.ops)
        while remaining:
            best = None
            for e in ENGS:
                lst = ready[e]
                if not lst:
                    continue
                ef = eng_free[e]
                c = None
                ck = None
                for o in lst:
                    r = o.ready + PE_SLACK if (e == "pe" and o.dur < 2.0) else o.ready
                    k = (max(r, ef), o.idx)
                    if ck is None or k < ck:
                        ck = k
                        c = o
                if best is None or ck < best[0]:
                    best = (ck, c)
            assert best is not None, "deadlock in schedule"
            (s, _), op = best
            ready[op.eng].remove(op)
            s = max(op.ready, eng_free[op.eng])
            op.start = s
            if op.dma:
                issue_end = s + 0.06
                dstart = max(issue_end, dma_free)
                dfin = dstart + op.dma_bytes / DMA_BW
                dma_free = dfin
                op.finish = dfin + DMA_LAT
                eng_free[op.eng] = issue_end
            else:
                op.finish = s + op.dur
                eng_free[op.eng] = op.finish
            order[op.eng].append(op)
            for su in op.succs:
                su.npred -= 1
                lat = 0.05 if (su.eng == op.eng and not op.dma) else 0.15
                if op.finish + lat > su.ready:
                    su.ready = op.finish + lat
                if su.npred == 0:
                    ready[su.eng].append(su)
            remaining -= 1
        self.order = order
        self.makespan = max(op.finish for op in self.ops)
        self.busy = {e: sum((o.dur if not o.dma else 0.06) for o in order[e]) for e in ENGS}
        for e in ENGS:
            cnt = 0
            for op in order[e]:
                if op.need_sig and not op.dma:
                    cnt += 1
                    op.sigval = cnt
        return order

    def emit(self, eng, handle, engsem):
        waited = {}
        for op in self.order[eng]:
            toks = {}
            for p, sync in op.preds.items():
                if not sync:
                    continue
                if p.dma:
                    sem, val = p.dma_sem, p.dma_val
                else:
                    sem, val = engsem[p.eng], p.sigval
                k = id(sem)
                if k not in toks or toks[k][1] < val:
                    toks[k] = (sem, val)
            for k, (sem, val) in toks.items():
                if waited.get(k, 0) < val:
                    handle.wait_ge(sem, val)
                    waited[k] = val
            inst = op.emit(handle)
            if op.dma:
                inst.then_inc(op.dma_sem, 16)
            elif op.need_sig:
                inst.then_inc(engsem[eng], 1)


def build_program(nseq, depth):
    ntok = nseq * 2048
    npass = nseq * 2
    nc = bass.Bass("TRN2", target_bir_lowering=False)
    xT = nc.dram_tensor("xT", [D, ntok], F32, kind="ExternalInput").ap()
    wA = nc.dram_tensor("wA", [depth * 160, 128, 2048], F32, kind="ExternalInput").ap()
    wD = nc.dram_tensor("wD", [depth * 64, 128, 1408], F32, kind="ExternalInput").ap()
    prm_d = nc.dram_tensor("prm", [128, NPRM], F32, kind="ExternalInput").ap()
    cst_d = nc.dram_tensor("cst", [128, 1152], F32, kind="ExternalInput").ap()
    yT = nc.dram_tensor("yT", [D, ntok], F32, kind="ExternalOutput").ap()
    sst = nc.dram_tensor("sst", [depth * 8, 128, 128], F32, kind="Internal").ap()

    S = Sched()
    st = contextlib.ExitStack()
    with st:
        def sb(name, shape, dt):
            return st.enter_context(nc.sbuf_tensor(name, shape, dt))

        xres = sb("xres", [128, KC, TP], F32)
        hT = sb("hT", [128, KC, TP], BF16)
        big = sb("big", [128, KC, TP], BF16)
        wbf = sb("wbf", [128, NBF, 2048], BF16)
        Ft = sb("Ft", [128, NF, FW], F32)
        Ht = sb("Ht", [128, NH, TT], BF16)
        vtok = sb("vtok", [64, 1024], BF16)
        khtok = sb("khtok", [64, 1024], BF16)
        ptb = sb("ptb", [64, TT], BF16)
        sbb = sb("sbb", [128, 8, 128], BF16)
        s32 = sb("s32", [128, 9, 128], F32)
        sm = sb("sm", [128, 32], F32)
        prm = sb("prm_sb", [128, NPRM], F32)
        lbw = sb("lbw", [128, 128], F32)
        rmask = sb("rmask", [128, TT], F32)
        cmask = sb("cmask", [64, TT], BF16)
        ident = sb("ident", [128, 128], BF16)
        ones = sb("ones", [128, 128], BF16)
        cvals = sb("cvals", [128, 4], F32)
        utail = sb("utail", [128, LFULL * 8 * 2], F32)
        pbs = [st.enter_context(nc.psum_tensor("pb%d" % k, [128, TT], F32)) for k in range(8)]

        engsem = {e: st.enter_context(nc.semaphore("es_" + e)) for e in ENGS}
        wsem = [st.enter_context(nc.semaphore("ws%d" % i)) for i in range(NBF)]
        xsem = [st.enter_context(nc.semaphore("xs%d" % i)) for i in range(KC)]
        osem = [st.enter_context(nc.semaphore("os%d" % i)) for i in range(4)]
        csem = [st.enter_context(nc.semaphore("cs%d" % i)) for i in range(4)]
        sldsem = st.enter_context(nc.semaphore("sld"))
        sstsem = st.enter_context(nc.semaphore("sst"))

        X = [[Buf("x%d_%d" % (i, t)) for t in range(2)] for i in range(KC)]
        HT = [Buf("hT%d" % t) for t in range(2)]
        BIG = [[Buf("big%d_%d" % (i, t)) for t in range(2)] for i in range(KC)]
        WBF = [Buf("wbf%d" % i) for i in range(NBF)]
        PB = [Buf("pb%d" % i) for i in range(8)]
        FB = [Buf("f%d" % i) for i in range(NF)]
        HB = [Buf("h%d" % i) for i in range(NH)]
        VTOK = Buf("vtok")
        KHTOK = Buf("khtok")
        PTB = Buf("ptb")
        SBB = Buf("sbb")
        S32B = [Buf("s32_%d" % i) for i in range(9)]
        SM = [Buf("sm%d" % i) for i in range(4)]
        UT = [[Buf("ut%d_%d" % (l, g)) for g in range(8)] for l in range(LFULL)]
        STD = [[Buf("std%d_%d" % (l, h)) for h in range(8)] for l in range(depth)]
        CONST = Buf("const")
        PRM = Buf("prm")
        LBW = Buf("lbw")

        def F(k, w=TT, o=0):
            return Ft[:, k, o:o + w]

        def H(k):
            return Ht[:, k, :]

        def pcol(c):
            return prm[:, c:c + 1]

        def act_dur(n):
            return 0.22 + n * 0.00083

        def dve_dur(n, two_sbuf=False):
            return 0.22 + n * 0.00104

        def ACT(out, in_, func, reads, writes, scale=None, bias=None, n=TT):
            kw = {}
            if scale is not None:
                kw["scale"] = scale
            if bias is not None:
                kw["bias"] = bias
            S.add("act", lambda e: e.activation(out=out, in_=in_, func=func, **kw), reads, writes, dur=act_dur(n))

        def TTENS(out, in0, in1, op, reads, writes, n=TT, two_sbuf=False, eng="dve"):
            S.add(eng, lambda e: e.tensor_tensor(out=out, in0=in0, in1=in1, op=op), reads, writes,
                  dur=dve_dur(n, two_sbuf))

        def STT(out, in0, scalar, in1, op0, op1, reads, writes, n=TT, two_sbuf=False):
            S.add("dve", lambda e: e.scalar_tensor_tensor(out=out, in0=in0, scalar=scalar, in1=in1, op0=op0, op1=op1),
                  reads, writes, dur=dve_dur(n, two_sbuf))

        def TSC(out, in0, s1, s2, op0, op1, reads, writes, n=TT, eng="dve"):
            if op1 is None:
                S.add(eng, lambda e: e.tensor_scalar(out=out, in0=in0, scalar1=s1, scalar2=None, op0=op0),
                      reads, writes, dur=dve_dur(n))
            else:
                S.add(eng, lambda e: e.tensor_scalar(out=out, in0=in0, scalar1=s1, scalar2=s2, op0=op0, op1=op1),
                      reads, writes, dur=dve_dur(n))

        def COPY(eng, out, in_, reads, writes, n=TT):
            S.add(eng, lambda e: e.tensor_copy(out=out, in_=in_), reads, writes, dur=dve_dur(n))

        def MMG(out_ap, lhs_list, rhs_list, reads, writes, n=TT):
            k = len(lhs_list)

            def emit(e):
                inst = None
                for i in range(k):
                    inst = e.matmul(out_ap, lhs_list[i], rhs_list[i], start=(i == 0), stop=(i == k - 1))
                return inst
            S.add("pe", emit, reads, writes, dur=k * (max(n, 64) / 1950.0 + 0.002))

        def MM1(out_ap, lhs, rhs, start, stop, reads, writes, n=TT):
            S.add("pe", lambda e: e.matmul(out_ap, lhs, rhs, start=start, stop=stop), reads, writes,
                  dur=max(n, 64) / 1950.0 + 0.03)

        wctr = [0]

        def wget(src_ap, width):
            n = wctr[0]
            wctr[0] += 1
            s2 = n % NBF
            S.add("pool", lambda e: e.dma_start(out=wbf[:, s2, 0:width], in_=src_ap), [], [WBF[s2]],
                  dma_sem=wsem[s2], dma_bytes=128 * width * 4)
            return wbf[:, s2, :], WBF[s2]

        S.add("sp", lambda e: e.dma_start(out=prm[:], in_=prm_d), [], [PRM], dma_sem=csem[0], dma_bytes=128 * NPRM * 4)
        S.add("sp", lambda e: e.dma_start(out=rmask[:], in_=cst_d[:, 0:512]), [], [CONST], dma_sem=csem[1], dma_bytes=2 ** 18)
        S.add("sp", lambda e: e.dma_start(out=Ft[0:64, 0, 0:512], in_=cst_d[0:64, 512:1024]), [], [FB[0]], dma_sem=csem[2], dma_bytes=2 ** 17)
        S.add("sp", lambda e: e.dma_start(out=Ft[:, 1, 0:128], in_=cst_d[:, 1024:1152]), [], [FB[1]], dma_sem=csem[3], dma_bytes=2 ** 16)
        COPY("pool", cmask[:], Ft[0:64, 0, 0:512], [FB[0]], [CONST])
        COPY("pool", ident[:], Ft[:, 1, 0:128], [FB[1]], [CONST])
        S.add("pool", lambda e: e.memset(ones[:], 1.0), [], [CONST], dur=0.2)
        S.add("pool", lambda e: e.memset(cvals[:, 0:1], EPS), [], [CONST], dur=0.1)
        S.add("pool", lambda e: e.memset(cvals[:, 1:2], 1.0), [], [CONST], dur=0.1)
        eps_ap = cvals[:, 0:1]
        one_ap = cvals[:, 1:2]
        LBL = 304

        def lsl(l):
            return prm[:, LBL + l:LBL + 32:4]

        w = lambda a, b: lbw[:, a:b]
        TTENS(w(96, 104), lsl(0), lsl(1), ALU.max, [PRM], [LBW], n=8)
        TTENS(w(104, 112), lsl(2), lsl(3), ALU.max, [PRM], [LBW], n=8)
        TTENS(w(96, 104), w(96, 104), w(104, 112), ALU.max, [LBW], [LBW], n=8)
        for l in range(4):
            TTENS(w(l * 8, l * 8 + 8), lsl(l), w(96, 104), ALU.subtract, [PRM, LBW], [LBW], n=8)
        ACT(w(0, 32), w(0, 32), AF.Exp, [LBW, CONST], [LBW], n=32)
        TTENS(w(104, 112), w(0, 8), w(8, 16), ALU.add, [LBW], [LBW], n=8)
        TTENS(w(112, 120), w(16, 24), w(24, 32), ALU.add, [LBW], [LBW], n=8)
        TTENS(w(104, 112), w(104, 112), w(112, 120), ALU.add, [LBW], [LBW], n=8)
        S.add("dve", lambda e: e.reciprocal(out=w(104, 112), in_=w(104, 112)), [LBW], [LBW], dur=0.2)
        for l in range(1, 4):
            TTENS(w(l * 8, l * 8 + 8), w(l * 8, l * 8 + 8), w(104, 112), ALU.mult, [LBW], [LBW], n=8)
        TTENS(w(0, 8), w(0, 8), w(0, 8), ALU.subtract, [LBW], [LBW], n=8)
        TTENS(w(16, 24), w(8, 16), w(16, 24), ALU.add, [LBW], [LBW], n=8)
        TTENS(w(24, 32), w(16, 24), w(24, 32), ALU.add, [LBW], [LBW], n=8)
        TSC(w(0, 32), w(0, 32), 0.0, 1.0 - 1e-4, ALU.max, ALU.min, [LBW], [LBW], n=32)
        TSC(w(32, 64), w(0, 32), -1.0, 1.0, ALU.mult, ALU.add, [LBW], [LBW], n=32)
        ACT(w(64, 96), w(32, 64), AF.Ln, [LBW], [LBW], n=32)

        def lb_ap(l, hd):
            return lbw[:, l * 8 + hd:l * 8 + hd + 1]

        def lnoml_ap(l, hd):
            return lbw[:, 64 + l * 8 + hd:64 + l * 8 + hd + 1]

        sq_rr = [0]
        ft_rr = [0]

        def tsl_(tt):
            return slice(tt * TT, (tt + 1) * TT)

        def stat_accumulate(i, tt, first, last):
            k = sq_rr[0] % 4
            sq_rr[0] += 1
            ACT(H(k), xres[:, i, tsl_(tt)], AF.Square, [X[i][tt], CONST], [HB[k]])
            MM1(pbs[6 + tt][:], ones[:], H(k), first, last, [HB[k], CONST], [PB[6 + tt]])

        def rstd_to_psum(tt):
            ACT(F(8 + tt), pbs[6 + tt][:], AF.Ln, [PB[6 + tt], CONST], [FB[8 + tt]], scale=1.0 / D, bias=eps_ap)
            ACT(pbs[6 + tt][:], F(8 + tt), AF.Exp, [FB[8 + tt]], [PB[6 + tt]], scale=-0.5)

        def norm_finalize(gbase):
            for tt in range(2):
                rstd_to_psum(tt)
                for kc in range(KC):
                    STT(hT[:, kc, tsl_(tt)], xres[:, kc, tsl_(tt)], pcol(gbase + kc), pbs[6 + tt][:], ALU.mult, ALU.mult,
                        [X[kc][tt], PB[6 + tt], PRM], [HT[tt]])

        pair_rr = [0]

        def next_pair(npairs=2):
            p = pair_rr[0] % npairs
            pair_rr[0] += 1
            return 2 * p

        def pair_group(b0, wap, rhs, nk, reads):
            lhs = [wap[:, k * 128:(k + 1) * 128] for k in range(nk)]

            def emit(e):
                inst = None
                for k in range(nk):
                    e.matmul(pbs[b0][:], lhs[k], rhs(k, 0), start=(k == 0), stop=(k == nk - 1))
                    inst = e.matmul(pbs[b0 + 1][:], lhs[k], rhs(k, 1), start=(k == 0), stop=(k == nk - 1))
                return inst
            S.add("pe", emit, reads, [PB[b0], PB[b0 + 1]], dur=2 * nk * (TT / 2400.0 + 0.003))

        def single_group(bank, wap, rhs, nk, tt, reads):
            lhs = [wap[:, k * 128:(k + 1) * 128] for k in range(nk)]

            def emit(e):
                inst = None
                for k in range(nk):
                    inst = e.matmul(pbs[bank][:], lhs[k], rhs(k, tt), start=(k == 0), stop=(k == nk - 1))
                return inst
            S.add("pe", emit, reads, [PB[bank]], dur=nk * (TT / 2400.0 + 0.003))

        def h_rhs(k, tt):
            return hT[:, k, tsl_(tt)]

        def big_rhs(k, tt):
            return big[:, k, tsl_(tt)]

        def in_proj(blk_idx):
            wap, wbuf = wget(wA[blk_idx], 2048)
            b0 = next_pair()
            pair_group(b0, wap, h_rhs, KC, [wbuf, HT[0], HT[1]])
            return b0

        def bigf(m):
            return big[:, m, :].bitcast(F32), [BIG[m][0], BIG[m][1]]

        def early_tiles(hd, tt):
            if tt == 1 and hd % 2 == 1 and hd < 7:
                return [bigf(6), bigf(7), bigf(14), bigf(15)]
            if tt == 1 and hd == 6:
                return [bigf(7), bigf(15), (F(6), [FB[6]]), (F(7), [FB[7]])]
            return [(F(4 * tt + k), [FB[4 * tt + k]]) for k in range(4)]

        def hgrn_chain(l, hd, tt):
            lb = lb_ap(l, hd)
            lnoml = lnoml_ap(l, hd)
            gn = pcol(272 + l * 8 + hd)
            (Tz, Bz), (Tt, Bt), (Tq, Bq), (Tg, Bg) = early_tiles(hd, tt)
            ivb = 0 if tt == 0 else 6
            ACT(F(10), Tz, AF.Ln, Bz + [CONST], [FB[10]], bias=one_ap)
            ACT(F(8), Tz, AF.Ln, Bz + [LBW], [FB[8]], scale=lb, bias=one_ap)
            ACT(Tt, Tt, AF.Ln, Bt, Bt, bias=one_ap)
            TTENS(F(8), F(8), F(10), ALU.subtract, [FB[8], FB[10]], [FB[8]], two_sbuf=True)
            S.add("dve", lambda e: e.tensor_tensor_scan(out=Tz, data0=rmask[:], data1=F(8), initial=0.0,
                                                       op0=ALU.mult, op1=ALU.add),
                  [FB[8], CONST], Bz, dur=dve_dur(TT, True))
            b3 = Tz.rearrange("p (c t) -> p c t", t=64)
            bp3 = F(8).rearrange("p (c t) -> p c t", t=64)
            TTENS(bp3, b3, b3[:, :, 31:32].broadcast_to([128, 8, 64]), ALU.subtract, Bz, [FB[8]],
                  two_sbuf=True)
            TTENS(Tt, Tt, F(8), ALU.add, Bt + [FB[8]], Bt, two_sbuf=True)
            ACT(H(1), Tt, AF.Exp, Bt + [LBW], [HB[1]], scale=-1.0, bias=lnoml)
            ACT(F(9), Tq, AF.Exp, Bq, [FB[9]], scale=-1.0)
            ACT(F(9), F(9), AF.Ln, [FB[9]], [FB[9]], bias=one_ap)
            ACT(F(9), F(9), AF.Exp, [FB[9]], [FB[9]], scale=-1.0)
            TTENS(Tq, Tq, F(9), ALU.mult, Bq + [FB[9]], Bq, two_sbuf=True)
            ACT(F(9), F(8), AF.Exp, [FB[8]] + Bq, [FB[9]])
            TTENS(H(2), Tq, F(9), ALU.mult, Bq + [FB[9]], [HB[2]], two_sbuf=True)
            ACT(F(11), Tg, AF.Exp, Bg, [FB[11]], scale=-1.0)
            ACT(F(11), F(11), AF.Ln, [FB[11]], [FB[11]], bias=one_ap)
            ACT(F(11), F(11), AF.Exp, [FB[11]], [FB[11]], scale=-1.0)
            TTENS(Tg, Tg, F(11), ALU.mult, Bg + [FB[11]], Bg, two_sbuf=True)
            ACT(sm[:, 0:8], b3[:, :, 31], AF.Exp, Bz, [SM[0]], n=8)
            ACT(sm[:, 8:16], b3[:, :, 63], AF.Exp, Bz, [SM[1]], n=8)
            TTENS(sm[:, 16:24], b3[:, :, 63], b3[:, :, 31], ALU.subtract, Bz, [SM[2]], n=8)
            ACT(sm[:, 24:32], sm[:, 16:24], AF.Exp, [SM[2]], [SM[3]], n=8)
            q3 = H(2).rearrange("p (c t) -> p c t", t=64)
            k3 = H(1).rearrange("p (c t) -> p c t", t=64)
            qh3 = H(3).rearrange("p (c t) -> p c t", t=64)
            kh3 = H(4).rearrange("p (c t) -> p c t", t=64)
            TTENS(qh3, q3, sm[:, 0:8].unsqueeze(2).broadcast_to([128, 8, 64]), ALU.mult, [HB[2], SM[0]], [HB[3]],
                  two_sbuf=True)
            TTENS(kh3, k3, sm[:, 24:32].unsqueeze(2).broadcast_to([128, 8, 64]), ALU.mult, [HB[1], SM[3]], [HB[4]],
                  two_sbuf=True)
            p4b = pbs[4][:].bitcast(BF16)
            p5b = pbs[5][:].bitcast(BF16)

            def tr_emit(dst, src):
                def emit(e):
                    inst = None
                    for c in range(8):
                        inst = e.transpose(out=dst[0:64, c * 128:(c + 1) * 128], in_=src[:, c * 64:(c + 1) * 64],
                                           identity=ident[:])
                    return inst
                return emit
            S.add("pe", tr_emit(p4b, H(ivb)), [HB[ivb], CONST], [PB[4]], dur=0.8)
            ACT(vtok[:], p4b[0:64, :], AF.Copy, [PB[4]], [VTOK], n=1024)
            S.add("pe", tr_emit(p5b, H(4)), [HB[4], CONST], [PB[5]], dur=0.8)
            COPY("dve", khtok[:], p5b[0:64, :], [PB[5]], [KHTOK], n=1024)

            def sc_emit(e):
                inst = None
                for c in range(8):
                    cs = slice(c * 64, (c + 1) * 64)
                    inst = e.matmul(pbs[6][0:64, cs], Ht[:, 1, cs], Ht[:, 2, cs], start=True, stop=True)
                return inst
            S.add("pe", sc_emit, [HB[1], HB[2]], [PB[6]], dur=0.6)
            TTENS(ptb[:], pbs[6][0:64, :], cmask[:], ALU.mult, [PB[6], CONST], [PTB])

            def ds_emit(bank, c0):
                def emit(e):
                    inst = None
                    for c in range(c0, c0 + 4):
                        inst = e.matmul(pbs[bank][:, (c - c0) * 128:(c - c0 + 1) * 128],
                                        khtok[:, c * 128:(c + 1) * 128], vtok[:, c * 128:(c + 1) * 128],
                                        start=True, stop=True)
                    return inst
                return emit
            S.add("pe", ds_emit(7, 0), [KHTOK, VTOK], [PB[7]], dur=0.5)
            S.add("pe", ds_emit(4, 4), [KHTOK, VTOK], [PB[4]], dur=0.5)
            for c in range(8):
                bank = 7 if c < 4 else 4
                cc = c % 4
                STT(s32[:, c + 1, :], s32[:, c, :], sm[:, 8 + c:9 + c], pbs[bank][:, cc * 128:(cc + 1) * 128],
                    ALU.mult, ALU.add, [S32B[c], SM[1], PB[bank]], [S32B[c + 1]], n=128)
            ACT(sbb[:], s32[:, 0:8, :], AF.Copy, [S32B[c] for c in range(8)], [SBB], n=1024)

            def o_emit(e):
                inst = None
                for c in range(8):
                    cs = slice(c * 64, (c + 1) * 64)
                    e.matmul(pbs[5][:, cs], vtok[:, c * 128:(c + 1) * 128], ptb[:, cs], start=True, stop=False)
                    inst = e.matmul(pbs[5][:, cs], sbb[:, c, :], Ht[:, 3, cs], start=False, stop=True)
                return inst
            S.add("pe", o_emit, [VTOK, PTB, SBB, HB[3]], [PB[5]], dur=1.2)
            ACT(H(5), pbs[5][:], AF.Square, [PB[5]], [HB[5]])
            MM1(pbs[6][:], ones[:], H(5), True, True, [HB[5], CONST], [PB[6]])
            ACT(F(10), pbs[6][:], AF.Ln, [PB[6], CONST], [FB[10]], scale=1.0 / 128, bias=eps_ap)
            ACT(F(10), F(10), AF.Exp, [FB[10]], [FB[10]], scale=-0.5)
            STT(F(10), pbs[5][:], gn, F(10), ALU.mult, ALU.mult, [PB[5], FB[10], PRM], [FB[10]])
            TTENS(big[:, 8 + hd, tsl_(tt)], F(10), Tg, ALU.mult, [FB[10]] + Bg, [BIG[8 + hd][tt]], two_sbuf=True)
            if tt == 0:
                COPY("dve", s32[:, 0, :], s32[:, 8, :], [S32B[8]], [S32B[0]], n=128)

        def hgrn_unit(l, hd, half):
            base = l * 160 + hd * 7
            if half == 0:
                S.add("dve", lambda e: e.memset(s32[:, 0, :], 0.0), [], [S32B[0]], dur=0.2)
            else:
                S.add("sp", lambda e: e.dma_start(out=s32[:, 0, :], in_=sst[l * 8 + hd]), [STD[l][hd]], [S32B[0]],
                      dma_sem=sldsem, dma_bytes=2 ** 16)
            et = [early_tiles(hd, 0), early_tiles(hd, 1)]
            if hd == 0:
                wq = wget(wA[base + 0], 2048)
                wz = wget(wA[base + 1], 2048)
                bq = next_pair()
                bz = next_pair()
                for tt in range(2):
                    single_group(bq + tt, wq[0], h_rhs, KC, tt, [wq[1], HT[tt]])
                    single_group(bz + tt, wz[0], h_rhs, KC, tt, [wz[1], HT[tt]])
                b0q, b0z = bq, bz
            else:
                b0q = in_proj(base + 0)
            b0 = b0q
            for tt in range(2):
                ACT(et[tt][2][0], pbs[b0 + tt][:], AF.Copy, [PB[b0 + tt]], et[tt][2][1])
            b0 = b0z if hd == 0 else in_proj(base + 1)
            for tt in range(2):
                ACT(et[tt][0][0], pbs[b0 + tt][:], AF.Exp, [PB[b0 + tt], CONST], et[tt][0][1], scale=-1.0)
                ACT(et[tt][1][0], pbs[b0 + tt][:], AF.Exp, [PB[b0 + tt]], et[tt][1][1])
            b0 = in_proj(base + 2)
            for tt in range(2):
                ivb = 0 if tt == 0 else 6
                ACT(H(ivb), pbs[b0 + tt][:], AF.Copy, [PB[b0 + tt]], [HB[ivb]])
            b0 = in_proj(base + 3)
            for tt in range(2):
                ACT(et[tt][3][0], pbs[b0 + tt][:], AF.Copy, [PB[b0 + tt]], et[tt][3][1])
            for tt in range(2):
                hgrn_chain(l, hd, tt)
            if half == 0:
                S.add("sp", lambda e: e.dma_start(out=sst[l * 8 + hd], in_=s32[:, 8, :]), [S32B[8]], [STD[l][hd]],
                      dma_sem=sstsem, dma_bytes=2 ** 16)

        def conv_unit(l, g):
            base = l * 160 + g * 7 + 4
            cw = 144 + (l * 8 + g) * 3
            gn = pcol(240 + l * 8 + g)
            ut = utail[:, (l * 8 + g) * 2:(l * 8 + g) * 2 + 2]
            IA = (12, 13)
            IU = (14, 15)
            b0 = in_proj(base + 2)
            for tt in range(2):
                ACT(F(IA[tt]), pbs[b0 + tt][:], AF.Copy, [PB[b0 + tt]], [FB[IA[tt]]])
            b0 = in_proj(base + 1)
            for tt in range(2):
                iu = IU[tt]
                if tt == 0:
                    ACT(Ft[:, iu, 0:2], ut, AF.Copy, [UT[l][g]], [FB[iu]], n=2)
                TTENS(Ft[:, iu, 2:514], pbs[b0 + tt][:], F(IA[tt]), ALU.mult, [PB[b0 + tt], FB[IA[tt]]], [FB[iu]])
                if tt == 0:
                    ACT(Ft[:, IU[1], 0:2], Ft[:, IU[0], 512:514], AF.Copy, [FB[IU[0]]], [FB[IU[1]]], n=2)
                else:
                    ACT(ut, Ft[:, IU[1], 512:514], AF.Copy, [FB[IU[1]]], [UT[l][g]], n=2)
            b0 = in_proj(base + 0)
            for tt in range(2):
                iu, ia = IU[tt], IA[tt]
                TSC(F(ia), Ft[:, iu, 0:512], pcol(cw), None, ALU.mult, None, [FB[iu], PRM], [FB[ia]])
                STT(F(ia), Ft[:, iu, 1:513], pcol(cw + 1), F(ia), ALU.mult, ALU.add, [FB[iu], FB[ia], PRM], [FB[ia]])
                STT(F(ia), Ft[:, iu, 2:514], pcol(cw + 2), F(ia), ALU.mult, ALU.add, [FB[iu], FB[ia], PRM], [FB[ia]])
                TTENS(F(ia), pbs[b0 + tt][:], F(ia), ALU.mult, [PB[b0 + tt], FB[ia]], [FB[ia]])
                ACT(H(5), F(ia), AF.Square, [FB[ia]], [HB[5]])
                MM1(pbs[6][:], ones[:], H(5), True, True, [HB[5], CONST], [PB[6]])
                ACT(F(iu), pbs[6][:], AF.Ln, [PB[6], CONST], [FB[iu]], scale=1.0 / 128, bias=eps_ap)
                ACT(F(iu), F(iu), AF.Exp, [FB[iu]], [FB[iu]], scale=-0.5)
                STT(big[:, g, tsl_(tt)], F(ia), gn, F(iu), ALU.mult, ALU.mult, [FB[ia], FB[iu], PRM], [BIG[g][tt]])

        def out_proj(l):
            base = l * 160 + 56
            for i in range(KC):
                wap, wbuf = wget(wA[base + i], 2048)
                b0 = next_pair()
                pair_group(b0, wap, big_rhs, KC, [wbuf] + [BIG[kc][tt] for kc in range(KC) for tt in range(2)])
                for tt in range(2):
                    TTENS(xres[:, i, tsl_(tt)], pbs[b0 + tt][:], xres[:, i, tsl_(tt)], ALU.add, [PB[b0 + tt], X[i][tt]], [X[i][tt]])
                    stat_accumulate(i, tt, i == 0, i == KC - 1)

        def ffn(l):
            gbase = l * 160 + 72
            for qd in range(NQ):
                pre = {}
                if qd == 0:
                    ws = [wget(wA[gbase + k], 2048) for k in range(4)]
                    bs = [next_pair(4) for k in range(4)]
                    for tt in range(2):
                        for k in range(4):
                            single_group(bs[k] + tt, ws[k][0], h_rhs, KC, tt, [ws[k][1], HT[tt]])
                    pre = {0: (bs[0], bs[1]), 1: (bs[2], bs[3])}
                for j in range(JQ):
                    jj = qd * JQ + j
                    if j in pre:
                        bg, bu = pre[j]
                    else:
                        wg = wget(wA[gbase + 2 * jj], 2048)
                        bg = next_pair(4)
                        pair_group(bg, wg[0], h_rhs, KC, [wg[1], HT[0], HT[1]])
                        wu = wget(wA[gbase + 2 * jj + 1], 2048)
                        bu = next_pair(4)
                        pair_group(bu, wu[0], h_rhs, KC, [wu[1], HT[0], HT[1]])
                    for tt in range(2):
                        k = ft_rr[0] % 4
                        ft_rr[0] += 1
                        ACT(F(k), pbs[bg + tt][:], AF.Silu, [PB[bg + tt]], [FB[k]])
                        TTENS(big[:, j, tsl_(tt)], pbs[bu + tt][:], F(k), ALU.mult, [PB[bu + tt], FB[k]], [BIG[j][tt]])
                for i in range(KC):
                    wap, wbuf = wget(wD[l * 64 + qd * 16 + i], 1408)
                    b0 = next_pair(3)
                    pair_group(b0, wap, big_rhs, JQ, [wbuf] + [BIG[j][tt] for j in range(JQ) for tt in range(2)])
                    for tt in range(2):
                        TTENS(xres[:, i, tsl_(tt)], pbs[b0 + tt][:], xres[:, i, tsl_(tt)], ALU.add, [PB[b0 + tt], X[i][tt]], [X[i][tt]])
                        if qd == NQ - 1:
                            stat_accumulate(i, tt, i == 0, i == KC - 1)

        for ps in range(npass):
            seq, half = divmod(ps, 2)
            tok0 = seq * 2048 + half * TP
            for i in range(KC):
                S.add("sp", lambda e, i=i, tok0=tok0: e.dma_start(out=xres[:, i, :], in_=xT[i * 128:(i + 1) * 128, tok0:tok0 + TP]),
                      [], [X[i][0], X[i][1]], dma_sem=xsem[i], dma_bytes=128 * TP * 4)
            if half == 0:
                S.add("dve", lambda e: e.memset(utail[:], 0.0), [], [UT[l][g] for l in range(LFULL) for g in range(8)], dur=0.2)
            for tt in range(2):
                for i in range(KC):
                    stat_accumulate(i, tt, i == 0, i == KC - 1)
            for l in range(depth):
                norm_finalize(l * 16)
                for u in range(8):
                    hgrn_unit(l, u, half)
                    conv_unit(l, u)
                out_proj(l)
                norm_finalize(64 + l * 16)
                ffn(l)
            for tt in range(2):
                rstd_to_psum(tt)
                for i in range(KC):
                    k = ft_rr[0] % 4
                    ft_rr[0] += 1
                    STT(F(k), xres[:, i, tsl_(tt)], pcol(128 + i), pbs[6 + tt][:], ALU.mult, ALU.mult,
                        [X[i][tt], PB[6 + tt], PRM], [FB[k]])
                    S.add("sp", lambda e, i=i, k=k, t0=tok0 + tt * TT: e.dma_start(out=yT[i * 128:(i + 1) * 128, t0:t0 + TT], in_=F(k)),
                          [FB[k]], [], dma_sem=osem[k], dma_bytes=128 * TT * 4)

        S.schedule()
        build_program.last_sched = S
        final_waits = [(osem[k], S.dma_counts.get(id(osem[k]), 0) * 16) for k in range(4)]

        with nc.Block() as block:
            @block.tensor
            def _(e):
                S.emit("pe", e, engsem)

            @block.scalar
            def _(e):
                S.emit("act", e, engsem)

            @block.vector
            def _(e):
                S.emit("dve", e, engsem)

            @block.gpsimd
            def _(e):
                S.emit("pool", e, engsem)

            @block.sync
            def _(e):
                S.emit("sp", e, engsem)
                for sem, val in final_waits:
                    if val:
                        e.wait_ge(sem, val)
    return nc


def _prep_weights(w_in, w_out, w_gate, w_up, w_down, depth):
    wA = np.empty((depth * 160, 128, 2048), np.float32)
    wD = np.empty((depth * 64, 128, 1408), np.float32)
    order = []
    for u in range(8):
        order += [24 + u, 32 + u, 40 + u, 48 + u, u, 8 + u, 16 + u]
    for l in range(depth):
        wi = np.asarray(w_in[l]).reshape(16, 128, 56, 128).transpose(2, 1, 0, 3).reshape(56, 128, 2048)
        wA[l * 160:l * 160 + 56] = wi[order]
        wA[l * 160 + 56:l * 160 + 72] = np.asarray(w_out[l]).reshape(16, 128, 16, 128).transpose(2, 1, 0, 3).reshape(16, 128, 2048)
        wg = np.asarray(w_gate[l]).reshape(16, 128, 44, 128).transpose(2, 1, 0, 3).reshape(44, 128, 2048)
        wu = np.asarray(w_up[l]).reshape(16, 128, 44, 128).transpose(2, 1, 0, 3).reshape(44, 128, 2048)
        wA[l * 160 + 72:l * 160 + 160:2] = wg
        wA[l * 160 + 73:l * 160 + 160:2] = wu
        wD[l * 64:(l + 1) * 64] = np.asarray(w_down[l]).reshape(4, 11, 128, 16, 128).transpose(0, 3, 2, 1, 4).reshape(64, 128, 1408)
    return wA, wD


def _prep_params(norm_mix, conv_w, gn_conv, lb_logits, gn_hgrn, norm_ffn, norm_final):
    prm = np.zeros((128, NPRM), np.float32)
    L = norm_mix.shape[0]
    for l in range(L):
        prm[:, l * 16:(l + 1) * 16] = np.asarray(norm_mix[l]).reshape(16, 128).T
        prm[:, 64 + l * 16:64 + (l + 1) * 16] = np.asarray(norm_ffn[l]).reshape(16, 128).T
        prm[:, 144 + l * 24:144 + (l + 1) * 24] = np.asarray(conv_w[l]).reshape(3, 8, 128).transpose(2, 1, 0).reshape(128, 24)
        prm[:, 240 + l * 8:240 + (l + 1) * 8] = np.asarray(gn_conv[l]).reshape(8, 128).T
        prm[:, 272 + l * 8:272 + (l + 1) * 8] = np.asarray(gn_hgrn[l]).reshape(8, 128).T
    prm[:, 128:144] = np.asarray(norm_final).reshape(16, 128).T
    prm[:, 304:336] = np.asarray(lb_logits).reshape(4, 8, 128).transpose(2, 1, 0).reshape(128, 32)
    return prm


def _consts():
    c = np.zeros((128, 1152), np.float32)
    c[:, 0:512] = 1.0
    c[:, 0:512:64] = 0.0
    cm = (np.arange(64)[:, None] <= np.arange(64)[None, :]).astype(np.float32)
    c[0:64, 512:1024] = np.tile(cm, (1, 8))
    c[:, 1024:1152] = np.eye(128, dtype=np.float32)
    return c


def kernel_impl(inputs, ncores, nseq, depth):
    x = np.asarray(inputs["x"], dtype=np.float32)
    wA, wD = _prep_weights(inputs["w_in"], inputs["w_out"], inputs["w_gate"], inputs["w_up"], inputs["w_down"], depth)
    prm = _prep_params(np.asarray(inputs["norm_mix"]), np.asarray(inputs["conv_w"]), np.asarray(inputs["gn_conv"]),
                       np.asarray(inputs["lb_logits"]), np.asarray(inputs["gn_hgrn"]), np.asarray(inputs["norm_ffn"]),
                       np.asarray(inputs["norm_final"]))
    cst = _consts()
    nc = build_program(nseq, depth)
    in_maps = []
    for c in range(ncores):
        xc = x[c * nseq:(c + 1) * nseq].reshape(nseq * 2048, D)
        in_maps.append({"xT": np.ascontiguousarray(xc.T), "wA": wA, "wD": wD, "prm": prm, "cst": cst})
    res = run_bass_kernel_spmd(nc, in_maps, core_ids=list(range(ncores)))
    out = np.empty((ncores * nseq, 2048, D), np.float32)
    for c in range(ncores):
        yT = res.results[c]["yT"]
        out[c * nseq:(c + 1) * nseq] = np.ascontiguousarray(yT.T).reshape(nseq, 2048, D)
    return out


def kernel(**inputs):
    return kernel_impl(inputs, 8, 2, 4)
```
